# Optimizing a Trainium2 kernel written in Bass

```python
import math
import jax
import jax.numpy as jnp
from jax import lax
import numpy as np

D_MODEL = 1024
BATCH = 2
SEQ = 16384
DEPTH = 2

CHUNK = 64
N_BRANCH = 4
MIX_WIDTH = D_MODEL // 4
CONV_W = 4
EPS = 1e-6

GDN_HEADS = 4
GDN_HEAD_DIM = MIX_WIDTH // GDN_HEADS

LRU_BLOCKS = 4
LRU_BLOCK = MIX_WIDTH // LRU_BLOCKS
LRU_C = 8.0

S5_GROUP = 16
S5_GROUPS = MIX_WIDTH // S5_GROUP
S5_STATE = 64

RWKV_HEADS = 4
RWKV_HEAD_DIM = MIX_WIDTH // RWKV_HEADS
DECAY_LORA = 64
AAA_LORA = 64
GATE_LORA = 128
RWKV_LN_EPS = 64e-5

D_FF = 4 * D_MODEL

GDN_COLS = 4 * MIX_WIDTH + 2 * GDN_HEADS
LRU_COLS = 2 * MIX_WIDTH
S5_COLS = MIX_WIDTH
RWKV_COLS = 3 * MIX_WIDTH + DECAY_LORA + AAA_LORA + GATE_LORA
GATE_COLS = N_BRANCH * D_MODEL
D_IN = GDN_COLS + LRU_COLS + S5_COLS + RWKV_COLS + GATE_COLS

kernel_name = "hybrid_gdn_rglru_s5_rwkv7_block"


def _split(t, widths):
    parts, off = [], 0
    for w in widths:
        parts.append(t[..., off:off + w])
        off += w
    return parts


def _rmsnorm(x, g):
    x32 = x.astype(jnp.float32)
    y = x32 * lax.rsqrt(jnp.mean(x32 * x32, axis=-1, keepdims=True) + EPS)
    return (y * g.astype(jnp.float32)).astype(x.dtype)


def _l2norm(t):
    return t * lax.rsqrt(jnp.sum(t * t, axis=-1, keepdims=True) + EPS)


def _causal_dwconv(u, w):
    k, c = w.shape
    return lax.conv_general_dilated(
        u, w[:, None, :].astype(u.dtype), window_strides=(1,), padding=[(k - 1, 0)],
        dimension_numbers=("NWC", "WIO", "NWC"), feature_group_count=c)


def _linear_recurrence(a, b):
    def combine(left, right):
        return left[0] * right[0], right[0] * left[1] + right[1]
    return lax.associative_scan(combine, (a, b), axis=1)[1]


def _chunk_gated_delta(q, k, v, g, beta):
    b, s, h, dk = q.shape
    dv = v.shape[-1]
    n = s // CHUNK

    def to_chunks(t):
        t = t.reshape((b, n, CHUNK, h) + t.shape[3:])
        return jnp.moveaxis(t, 3, 1)

    q, k, v, g, beta = (to_chunks(t) for t in (q, k, v, g, beta))
    gc = jnp.cumsum(g, axis=-1)
    idx = jnp.arange(CHUNK)
    causal = idx[:, None] >= idx[None, :]
    strict = idx[:, None] > idx[None, :]
    diff = gc[..., :, None] - gc[..., None, :]
    decay = jnp.where(causal, jnp.exp(jnp.where(causal, diff, 0.0)), 0.0)
    kb = k * beta[..., None]
    lmat = jnp.where(strict, jnp.einsum("bhnid,bhnjd->bhnij", kb, k) * decay, 0.0)
    eye = jnp.eye(CHUNK, dtype=q.dtype)
    rhs = jnp.concatenate([v * beta[..., None], kb * jnp.exp(gc)[..., None]], axis=-1)
    sol = lax.linalg.triangular_solve(eye + lmat, rhs, left_side=True, lower=True)
    u, w = sol[..., :dv], sol[..., dv:]
    attn = jnp.einsum("bhnid,bhnjd->bhnij", q, k) * decay
    q_dec = q * jnp.exp(gc)[..., None]
    g_last = gc[..., -1]
    k_dec = k * jnp.exp(g_last[..., None] - gc)[..., None]

    def step(state, inp):
        u_i, w_i, q_i, k_i, a_i, gl_i = inp
        v_new = u_i - jnp.einsum("bhck,bhkv->bhcv", w_i, state)
        o = jnp.einsum("bhck,bhkv->bhcv", q_i, state) + jnp.einsum("bhcj,bhjv->bhcv", a_i, v_new)
        state = state * jnp.exp(gl_i)[..., None, None] + jnp.einsum("bhck,bhcv->bhkv", k_i, v_new)
        return state, o

    xs = tuple(jnp.moveaxis(t, 2, 0) for t in (u, w, q_dec, k_dec, attn, g_last))
    state0 = jnp.zeros((b, h, dk, dv), q.dtype)
    _, o = lax.scan(step, state0, xs)
    o = jnp.moveaxis(o, 0, 2)
    return jnp.moveaxis(o, 1, 3).reshape(b, s, h, dv)


def _gdn_mixer(p, conv_w, a_log, dt_bias, norm_g):
    b, s, _ = p.shape
    f32 = jnp.float32
    qkv, z, beta_logit, a_logit = _split(p, (3 * MIX_WIDTH, MIX_WIDTH, GDN_HEADS, GDN_HEADS))
    qkv = jax.nn.silu(_causal_dwconv(qkv, conv_w)).astype(f32)
    q, k, v = (t.reshape(b, s, GDN_HEADS, GDN_HEAD_DIM) for t in _split(qkv, (MIX_WIDTH,) * 3))
    q = _l2norm(q) * (GDN_HEAD_DIM ** -0.5)
    k = _l2norm(k)
    beta = jax.nn.sigmoid(beta_logit.astype(f32))
    g = -jnp.exp(a_log.astype(f32)) * jax.nn.softplus(a_logit.astype(f32) + dt_bias.astype(f32))
    o = _chunk_gated_delta(q, k, v, g, beta)
    o = o * lax.rsqrt(jnp.mean(o * o, axis=-1, keepdims=True) + EPS) * norm_g.astype(f32)
    o = o * jax.nn.silu(z.astype(f32)).reshape(b, s, GDN_HEADS, GDN_HEAD_DIM)
    return o.reshape(b, s, MIX_WIDTH).astype(p.dtype)


def _rglru_mixer(p, conv_w, conv_b, w_a, b_a, w_x, b_x, lam):
    b, s, _ = p.shape
    f32 = jnp.float32
    xb, gate = _split(p, (MIX_WIDTH, MIX_WIDTH))
    u = (_causal_dwconv(xb, conv_w) + conv_b).astype(f32)
    ub = u.reshape(b, s, LRU_BLOCKS, LRU_BLOCK)
    r = jax.nn.sigmoid(jnp.einsum("bsnd,nde->bsne", ub, w_a.astype(f32)).reshape(b, s, MIX_WIDTH) + b_a.astype(f32))
    i = jax.nn.sigmoid(jnp.einsum("bsnd,nde->bsne", ub, w_x.astype(f32)).reshape(b, s, MIX_WIDTH) + b_x.astype(f32))
    log_a = -LRU_C * r * jax.nn.softplus(-lam.astype(f32))
    a = jnp.exp(log_a)
    inp = jnp.sqrt(-jnp.expm1(2.0 * log_a)) * (i * u)
    h = _linear_recurrence(a, inp)
    return (h * jax.nn.gelu(gate.astype(f32))).astype(p.dtype)


def _s5_mixer(u, lam_re, lam_im, b_re, b_im, c_re, c_im, d, log_dt, glu_w, glu_b):
    b, s, _ = u.shape
    f32 = jnp.float32
    ug = u.astype(f32).reshape(b, s, S5_GROUPS, S5_GROUP)
    lam = lax.complex(lam_re.astype(f32), lam_im.astype(f32))
    dt = jnp.exp(log_dt.astype(f32))[:, None]
    lam_bar = jnp.exp(lam * dt)
    b_bar = ((lam_bar - 1.0) / lam)[..., None] * lax.complex(b_re.astype(f32), b_im.astype(f32))
    bu = jnp.einsum("gpc,bsgc->bsgp", b_bar, ug.astype(jnp.complex64))
    h = _linear_recurrence(jnp.broadcast_to(lam_bar, bu.shape), bu)
    c = lax.complex(c_re.astype(f32), c_im.astype(f32))
    y = jnp.einsum("gcp,bsgp->bsgc", c, h).real + d.astype(f32).reshape(S5_GROUPS, S5_GROUP) * ug
    y = jax.nn.gelu(y.reshape(b, s, MIX_WIDTH))
    y = y * jax.nn.sigmoid(y @ glu_w.astype(f32) + glu_b.astype(f32))
    return y.astype(u.dtype)


def _rwkv7_mixer(p, mu, w0, w_up, a0, a_up, g_up, k_k, k_a, r_k, ln_g, ln_b):
    b, s, _ = p.shape
    f32 = jnp.float32
    dtype = p.dtype
    p = p.astype(f32)
    prev = jnp.pad(p, ((0, 0), (1, 0), (0, 0)))[:, :-1]
    p = p + mu.astype(f32) * (prev - p)
    r, k, v, wd, ad, gd = _split(p, (MIX_WIDTH,) * 3 + (DECAY_LORA, AAA_LORA, GATE_LORA))
    logw = -jax.nn.softplus(-(w0.astype(f32) + jnp.tanh(wd) @ w_up.astype(f32))) - 0.5
    decay = jnp.exp(-jnp.exp(logw))
    a = jax.nn.sigmoid(a0.astype(f32) + ad @ a_up.astype(f32))
    g = jax.nn.sigmoid(gd) @ g_up.astype(f32)

    def heads(t):
        return t.reshape(b, s, RWKV_HEADS, RWKV_HEAD_DIM)

    kk = _l2norm(heads(k * k_k.astype(f32)))
    k = k * (1.0 + (a - 1.0) * k_a.astype(f32))
    r, k, v, decay, a = heads(r), heads(k), heads(v), heads(decay), heads(a)

    def step(state, inp):
        r_t, w_t, k_t, v_t, kk_t, a_t = inp
        removed = jnp.einsum("bhvk,bhk->bhv", state, kk_t)
        state = (state * w_t[:, :, None, :]
                 - jnp.einsum("bhv,bhk->bhvk", removed, kk_t * a_t)
                 + jnp.einsum("bhv,bhk->bhvk", v_t, k_t))
        return state, jnp.einsum("bhvk,bhk->bhv", state, r_t)

    xs = tuple(jnp.moveaxis(t, 1, 0) for t in (r, decay, k, v, kk, a))
    state0 = jnp.zeros((b, RWKV_HEADS, RWKV_HEAD_DIM, RWKV_HEAD_DIM), f32)
    _, o = lax.scan(step, state0, xs)
    o = jnp.moveaxis(o, 0, 1)
    mean = jnp.mean(o, axis=-1, keepdims=True)
    var = jnp.mean(jnp.square(o - mean), axis=-1, keepdims=True)
    o = ((o - mean) * lax.rsqrt(var + RWKV_LN_EPS)).reshape(b, s, MIX_WIDTH)
    o = o * ln_g.astype(f32) + ln_b.astype(f32)
    bonus = jnp.sum(r * k * r_k.astype(f32), axis=-1, keepdims=True) * v
    o = (o + bonus.reshape(b, s, MIX_WIDTH)) * g
    return o.astype(dtype)


def _hybrid_layer(x, norm1_g, w_in,
                  gdn_conv_w, gdn_a_log, gdn_dt_bias, gdn_norm_g,
                  lru_conv_w, lru_conv_b, lru_w_a, lru_b_a, lru_w_x, lru_b_x, lru_lambda,
                  s5_lambda_re, s5_lambda_im, s5_b_re, s5_b_im, s5_c_re, s5_c_im,
                  s5_d, s5_log_dt, s5_glu_w, s5_glu_b,
                  rwkv_mu, rwkv_w0, rwkv_w_up, rwkv_a0, rwkv_a_up, rwkv_g_up,
                  rwkv_k_k, rwkv_k_a, rwkv_r_k, rwkv_ln_g, rwkv_ln_b,
                  w_branch, w_out, norm2_g, mlp_w1, mlp_w2):
    b, s, _ = x.shape
    h = _rmsnorm(x, norm1_g)
    proj = h @ w_in
    p_gdn, p_lru, p_s5, p_rwkv, p_gate = _split(proj, (GDN_COLS, LRU_COLS, S5_COLS, RWKV_COLS, GATE_COLS))
    ys = (
        _gdn_mixer(p_gdn, gdn_conv_w, gdn_a_log, gdn_dt_bias, gdn_norm_g),
        _rglru_mixer(p_lru, lru_conv_w, lru_conv_b, lru_w_a, lru_b_a, lru_w_x, lru_b_x, lru_lambda),
        _s5_mixer(p_s5, s5_lambda_re, s5_lambda_im, s5_b_re, s5_b_im, s5_c_re, s5_c_im,
                  s5_d, s5_log_dt, s5_glu_w, s5_glu_b),
        _rwkv7_mixer(p_rwkv, rwkv_mu, rwkv_w0, rwkv_w_up, rwkv_a0, rwkv_a_up, rwkv_g_up,
                     rwkv_k_k, rwkv_k_a, rwkv_r_k, rwkv_ln_g, rwkv_ln_b),
    )
    gate_logits = p_gate.reshape(b, s, N_BRANCH, D_MODEL)
    merged = None
    for i in range(N_BRANCH):
        term = jax.nn.sigmoid(gate_logits[:, :, i]) * (ys[i] @ w_branch[i])
        merged = term if merged is None else merged + term
    x = x + (merged @ w_out).astype(x.dtype)
    h2 = _rmsnorm(x, norm2_g)
    x = x + (jnp.square(jax.nn.relu(h2 @ mlp_w1)) @ mlp_w2).astype(x.dtype)
    return x


def setup_inputs(seed: int = 0) -> dict:
    key = jax.random.key(seed)
    keys = iter(jax.random.split(key, 64))
    f32 = jnp.float32

    def nrm(shape, scale):
        return scale * jax.random.normal(next(keys), shape, f32)

    def uni(shape, lo, hi):
        return jax.random.uniform(next(keys), shape, f32, lo, hi)

    L, W = DEPTH, MIX_WIDTH
    x = nrm((BATCH, SEQ, D_MODEL), 1.0)
    norm1_g = 1.0 + nrm((L, D_MODEL), 0.02)
    w_in = nrm((L, D_MODEL, D_IN), D_MODEL ** -0.5)
    gdn_conv_w = nrm((L, CONV_W, 3 * W), CONV_W ** -0.5)
    gdn_a_log = jnp.log(uni((L, GDN_HEADS), 1.0, 16.0))
    gdn_dt = jnp.exp(uni((L, GDN_HEADS), math.log(1e-3), math.log(1e-1)))
    gdn_dt_bias = gdn_dt + jnp.log(-jnp.expm1(-gdn_dt))
    gdn_norm_g = 1.0 + nrm((L, GDN_HEAD_DIM), 0.02)
    lru_conv_w = nrm((L, CONV_W, W), CONV_W ** -0.5)
    lru_conv_b = nrm((L, W), 0.01)
    lru_w_a = nrm((L, LRU_BLOCKS, LRU_BLOCK, LRU_BLOCK), LRU_BLOCK ** -0.5)
    lru_b_a = nrm((L, W), 0.01)
    lru_w_x = nrm((L, LRU_BLOCKS, LRU_BLOCK, LRU_BLOCK), LRU_BLOCK ** -0.5)
    lru_b_x = nrm((L, W), 0.01)
    a_base = uni((L, W), 0.9, 0.999) ** (1.0 / LRU_C)
    lru_lambda = jnp.log(a_base) - jnp.log1p(-a_base)
    s5_lambda_re = -0.5 + nrm((L, S5_GROUPS, S5_STATE), 0.01)
    s5_lambda_im = jnp.pi * jnp.arange(S5_STATE, dtype=f32) + nrm((L, S5_GROUPS, S5_STATE), 0.01)
    s5_b_re = nrm((L, S5_GROUPS, S5_STATE, S5_GROUP), (2 * S5_GROUP) ** -0.5)
    s5_b_im = nrm((L, S5_GROUPS, S5_STATE, S5_GROUP), (2 * S5_GROUP) ** -0.5)
    s5_c_re = nrm((L, S5_GROUPS, S5_GROUP, S5_STATE), S5_STATE ** -0.5)
    s5_c_im = nrm((L, S5_GROUPS, S5_GROUP, S5_STATE), S5_STATE ** -0.5)
    s5_d = nrm((L, W), 1.0)
    s5_log_dt = uni((L, S5_GROUPS), math.log(1e-3), math.log(1e-1))
    s5_glu_w = nrm((L, W, W), W ** -0.5)
    s5_glu_b = nrm((L, W), 0.01)
    rwkv_mu = uni((L, RWKV_COLS), 0.0, 1.0)
    rwkv_w0 = uni((L, W), -6.0, -1.0)
    rwkv_w_up = nrm((L, DECAY_LORA, W), 0.1)
    rwkv_a0 = nrm((L, W), 0.1)
    rwkv_a_up = nrm((L, AAA_LORA, W), 0.1)
    rwkv_g_up = nrm((L, GATE_LORA, W), GATE_LORA ** -0.5)
    rwkv_k_k = 0.85 + nrm((L, W), 0.02)
    rwkv_k_a = 1.0 + nrm((L, W), 0.02)
    rwkv_r_k = nrm((L, RWKV_HEADS, RWKV_HEAD_DIM), 0.1)
    rwkv_ln_g = 1.0 + nrm((L, W), 0.02)
    rwkv_ln_b = nrm((L, W), 0.01)
    w_branch = nrm((L, N_BRANCH, W, D_MODEL), W ** -0.5)
    w_out = nrm((L, D_MODEL, D_MODEL), D_MODEL ** -0.5)
    norm2_g = 1.0 + nrm((L, D_MODEL), 0.02)
    mlp_w1 = nrm((L, D_MODEL, D_FF), D_MODEL ** -0.5)
    mlp_w2 = nrm((L, D_FF, D_MODEL), D_FF ** -0.5)
    final_norm_g = 1.0 + nrm((D_MODEL,), 0.02)
    return {
        "x": x, "norm1_g": norm1_g, "w_in": w_in,
        "gdn_conv_w": gdn_conv_w, "gdn_a_log": gdn_a_log, "gdn_dt_bias": gdn_dt_bias, "gdn_norm_g": gdn_norm_g,
        "lru_conv_w": lru_conv_w, "lru_conv_b": lru_conv_b, "lru_w_a": lru_w_a, "lru_b_a": lru_b_a,
        "lru_w_x": lru_w_x, "lru_b_x": lru_b_x, "lru_lambda": lru_lambda,
        "s5_lambda_re": s5_lambda_re, "s5_lambda_im": s5_lambda_im, "s5_b_re": s5_b_re, "s5_b_im": s5_b_im,
        "s5_c_re": s5_c_re, "s5_c_im": s5_c_im, "s5_d": s5_d, "s5_log_dt": s5_log_dt,
        "s5_glu_w": s5_glu_w, "s5_glu_b": s5_glu_b,
        "rwkv_mu": rwkv_mu, "rwkv_w0": rwkv_w0, "rwkv_w_up": rwkv_w_up, "rwkv_a0": rwkv_a0,
        "rwkv_a_up": rwkv_a_up, "rwkv_g_up": rwkv_g_up, "rwkv_k_k": rwkv_k_k, "rwkv_k_a": rwkv_k_a,
        "rwkv_r_k": rwkv_r_k, "rwkv_ln_g": rwkv_ln_g, "rwkv_ln_b": rwkv_ln_b,
        "w_branch": w_branch, "w_out": w_out, "norm2_g": norm2_g, "mlp_w1": mlp_w1, "mlp_w2": mlp_w2,
        "final_norm_g": final_norm_g,
    }


def reference(x, norm1_g, w_in,
              gdn_conv_w, gdn_a_log, gdn_dt_bias, gdn_norm_g,
              lru_conv_w, lru_conv_b, lru_w_a, lru_b_a, lru_w_x, lru_b_x, lru_lambda,
              s5_lambda_re, s5_lambda_im, s5_b_re, s5_b_im, s5_c_re, s5_c_im,
              s5_d, s5_log_dt, s5_glu_w, s5_glu_b,
              rwkv_mu, rwkv_w0, rwkv_w_up, rwkv_a0, rwkv_a_up, rwkv_g_up,
              rwkv_k_k, rwkv_k_a, rwkv_r_k, rwkv_ln_g, rwkv_ln_b,
              w_branch, w_out, norm2_g, mlp_w1, mlp_w2, final_norm_g):
    for l in range(DEPTH):
        x = _hybrid_layer(
            x, norm1_g[l], w_in[l],
            gdn_conv_w[l], gdn_a_log[l], gdn_dt_bias[l], gdn_norm_g[l],
            lru_conv_w[l], lru_conv_b[l], lru_w_a[l], lru_b_a[l], lru_w_x[l], lru_b_x[l], lru_lambda[l],
            s5_lambda_re[l], s5_lambda_im[l], s5_b_re[l], s5_b_im[l], s5_c_re[l], s5_c_im[l],
            s5_d[l], s5_log_dt[l], s5_glu_w[l], s5_glu_b[l],
            rwkv_mu[l], rwkv_w0[l], rwkv_w_up[l], rwkv_a0[l], rwkv_a_up[l], rwkv_g_up[l],
            rwkv_k_k[l], rwkv_k_a[l], rwkv_r_k[l], rwkv_ln_g[l], rwkv_ln_b[l],
            w_branch[l], w_out[l], norm2_g[l], mlp_w1[l], mlp_w2[l])
    return _rmsnorm(x, final_norm_g)
```

```python
import numpy as np
from contextlib import ExitStack
import ml_dtypes
import concourse.bass as bass
import concourse.mybir as mybir
from concourse.bass_utils import run_bass_kernel_spmd

F32 = mybir.dt.float32
BF16 = mybir.dt.bfloat16
AF = mybir.ActivationFunctionType
ALU = mybir.AluOpType

D = 1024
SEQ = 16384
NB = 2
DEPTH = 2
W = 256
TT = 512
EPS = 1e-6


class Buf:
    def __init__(self, k, name, t):
        self.k, self.name, self.t = k, name, t
        self.lw = None
        self.rd = {}

    def __getitem__(self, idx):
        return self.t[idx]


class KB:
    ENG = {"pe": "tensor", "dve": "vector", "act": "scalar", "pool": "gpsimd", "sp": "sync"}

    def __init__(self):
        self.nc = bass.Bass("TRN2", target_bir_lowering=False)
        self.es = ExitStack()
        self.sems = {}
        self.cnt = {}
        self.waited = {e: {} for e in self.ENG}
        self.h = {e: getattr(self.nc, a) for e, a in self.ENG.items()}
        for e in self.ENG:
            self.sems[e] = self.es.enter_context(self.nc.semaphore("s_" + e))
            self.cnt[e] = 0
        self.nbuf = 0

    def sb(self, name, shape, dt=F32):
        return Buf(self, name, self.es.enter_context(self.nc.sbuf_tensor(name, list(shape), dt)))

    def ps(self, name, shape, dt=F32):
        return Buf(self, name, self.es.enter_context(self.nc.psum_tensor(name, list(shape), dt)))

    def dram(self, name, shape, dt, kind):
        return Buf(self, name, self.nc.dram_tensor(name, list(shape), dt, kind=kind).ap())

    def _dsem(self, b):
        n = "d_" + b.name
        if n not in self.sems:
            self.sems[n] = self.es.enter_context(self.nc.semaphore(n))
            self.cnt[n] = 0
        return n

    def _deps(self, eng, r, w):
        need = {}

        def req(d):
            if d is None:
                return
            s, c = d
            if need.get(s, 0) < c:
                need[s] = c
        for b in r:
            req(b.lw)
        for b in w:
            req(b.lw)
            for s, c in b.rd.items():
                req((s, c))
        wt = self.waited[eng]
        for s, c in need.items():
            if s == eng and eng == "pe":
                continue
            if wt.get(s, 0) < c:
                self.h[eng].wait_ge(self.sems[s], c)
                wt[s] = c

    def op(self, eng, fn, r=(), w=()):
        self._deps(eng, r, w)
        inst = fn(self.h[eng])
        self.cnt[eng] += 1
        c = self.cnt[eng]
        inst.then_inc(self.sems[eng], 1)
        for b in r:
            b.rd[eng] = c
        for b in w:
            b.lw = (eng, c)
            b.rd = {}

    def dma(self, q, out, in_, r=(), w=(), semb=None, **kw):
        self._deps(q, r, w)
        inst = self.h[q].dma_start(out=out, in_=in_, **kw)
        s = self._dsem(semb if semb is not None else (w[0] if not w[0].name.startswith("D_") else r[0]))
        self.cnt[s] += 16
        c = self.cnt[s]
        inst.then_inc(self.sems[s], 16)
        for b in r:
            b.rd[s] = c
        for b in w:
            b.lw = (s, c)
            b.rd = {}

    def sync_all(self, bufs):
        for eng in ("pe", "dve", "act"):
            self._deps(eng, bufs, ())

    def finish(self):
        for s, c in self.cnt.items():
            if c > 0 and s != "sp" and self.waited["sp"].get(s, 0) < c:
                self.h["sp"].wait_ge(self.sems[s], c)
        self.es.close()


def _tile_sz(n):
    return 32 if n <= 32 else (64 if n <= 64 else 128)


def mm(k, out_b, out_ap, lhs_b, lhs_ap, rhs_b, rhs_ap, start=True, stop=True):
    mode = (_tile_sz(lhs_ap.shape[0]), _tile_sz(int(np.prod(lhs_ap.shape[1:]))), lhs_ap.dtype == F32)
    if getattr(k, "pe_mode", None) not in (None, mode):
        k.h["pe"].drain()
    k.pe_mode = mode
    k.op("pe", lambda e: e.matmul(out_ap, lhsT=lhs_ap, rhs=rhs_ap, start=start, stop=stop),
         r=[lhs_b, rhs_b], w=[out_b])


def act(k, out_b, out_ap, in_b, in_ap, func, bias=None, scale=None, extra_r=()):
    kw = {}
    if bias is not None:
        kw["bias"] = bias
    if scale is not None:
        kw["scale"] = scale
    k.op("act", lambda e: e.activation(out=out_ap, in_=in_ap, func=func, **kw),
         r=[in_b, *extra_r], w=[out_b])


def tt(k, eng, out_b, out_ap, a_b, a_ap, b_b, b_ap, op):
    k.op(eng, lambda e: e.tensor_tensor(out=out_ap, in0=a_ap, in1=b_ap, op=op), r=[a_b, b_b], w=[out_b])


def ts(k, eng, out_b, out_ap, a_b, a_ap, s1, op0, s2=None, op1=None, extra_r=()):
    if op1 is None:
        k.op(eng, lambda e: e.tensor_scalar(out=out_ap, in0=a_ap, scalar1=s1, scalar2=None, op0=op0),
             r=[a_b, *extra_r], w=[out_b])
    else:
        k.op(eng, lambda e: e.tensor_scalar(out=out_ap, in0=a_ap, scalar1=s1, scalar2=s2, op0=op0, op1=op1),
             r=[a_b, *extra_r], w=[out_b])


def stt(k, out_b, out_ap, a_b, a_ap, s, b_b, b_ap, op0, op1, extra_r=()):
    k.op("dve", lambda e: e.scalar_tensor_tensor(out=out_ap, in0=a_ap, scalar=s, in1=b_ap, op0=op0, op1=op1),
         r=[a_b, b_b, *extra_r], w=[out_b])


def cp(k, eng, out_b, out_ap, in_b, in_ap):
    if eng == "act":
        k.op("act", lambda e: e.copy(out=out_ap, in_=in_ap), r=[in_b], w=[out_b])
    else:
        k.op(eng, lambda e: e.tensor_copy(out=out_ap, in_=in_ap), r=[in_b], w=[out_b])


def rsqrt_(k, out_b, out_ap, in_b, in_ap, tmp_b, tmp_ap, scale, bias_b, bias_ap):
    act(k, tmp_b, tmp_ap, in_b, in_ap, AF.Sqrt, bias=bias_ap, scale=scale, extra_r=[bias_b])
    k.op("dve", lambda e: e.reciprocal(out=out_ap, in_=tmp_ap), r=[tmp_b], w=[out_b])


def build_B(NT, mode):
    k = KB()
    nt = NT // TT
    xT = k.dram("D_xT", [D, NT], F32, "ExternalInput")
    gn = k.dram("D_gn", [D], F32, "ExternalInput")
    if mode != "pre":
        yT = k.dram("D_yT", [D, NT], BF16, "ExternalInput")
        wg = k.dram("D_wg", [D, 4 * D], F32, "ExternalInput")
        wb = k.dram("D_wb", [4, W, D], F32, "ExternalInput")
        wo = k.dram("D_wo", [D, D], F32, "ExternalInput")
        w1 = k.dram("D_w1", [D, 4 * D], F32, "ExternalInput")
        w2 = k.dram("D_w2", [4 * D, D], F32, "ExternalInput")
        gw = k.dram("D_gw", [W, W], F32, "ExternalInput")
        gb = k.dram("D_gb", [W], F32, "ExternalInput")
        g1 = k.dram("D_g1", [D], F32, "ExternalInput")
        g2 = k.dram("D_g2", [D], F32, "ExternalInput")
    if mode == "fin":
        oT = k.dram("D_oT", [D, NT], F32, "ExternalOutput")
    else:
        hT = k.dram("D_hT", [D, NT], BF16, "ExternalOutput")
        if mode == "mid":
            xo = k.dram("D_xo", [D, NT], F32, "ExternalOutput")

    ones = k.sb("ones", [128, 128], BF16)
    k.op("dve", lambda e: e.memset(ones[:], 1.0), w=[ones])
    epsc = k.sb("epsc", [128, 1])
    k.op("dve", lambda e: e.memset(epsc[:], EPS), w=[epsc])
    gcol = k.sb("gcol", [128, 4, 8])
    vec = lambda d_: d_[:].rearrange("(kt p) -> p kt", p=128)
    k.dma("sp", gcol[:, 2, :], vec(gn), r=[gn], w=[gcol], allow_slow_non_contiguous=True)
    if mode != "pre":
        k.dma("sp", gcol[:, 0, :], vec(g1), r=[g1], w=[gcol], allow_slow_non_contiguous=True)
        k.dma("sp", gcol[:, 1, :], vec(g2), r=[g2], w=[gcol], allow_slow_non_contiguous=True)
        gbc = k.sb("gbc", [128, 2])
        k.dma("sp", gbc[:, :], gb[:].rearrange("(kt p) -> p kt", p=128), r=[gb], w=[gbc],
              allow_slow_non_contiguous=True)

    x = [k.sb("x%d" % i, [128, 8, TT]) for i in range(2)]
    hb = k.sb("hb", [128, 8, TT], BF16)
    sqs = [k.sb("sq%d" % i, [128, TT], BF16) for i in range(4)]
    rstd = k.sb("rstd", [128, TT])
    rtmp = k.sb("rtmp", [128, TT])
    pss = [k.ps("ps%d" % i, [128, TT]) for i in range(8)]
    psi = [0]

    def nps():
        psi[0] = (psi[0] + 1) % 8
        return pss[psi[0]]

    def rmsnorm(src, dst, dst_dt_b, gi):
        p = nps()
        for kt in range(8):
            sq = sqs[kt % 4]
            act(k, sq, sq[:, :], src, src[:, kt, :], AF.Square)
            mm(k, p, p[:, :], ones, ones[:, :], sq, sq[:, :], start=(kt == 0), stop=(kt == 7))
        rsqrt_(k, rstd, rstd[:, :], p, p[:, :], rtmp, rtmp[:, :], 1.0 / D, epsc, epsc[:, 0:1])
        for kt in range(8):
            stt(k, dst, dst[:, kt, :], src, src[:, kt, :], gcol[:, gi, kt:kt + 1], rstd, rstd[:, :],
                ALU.mult, ALU.mult, extra_r=[gcol])

    if mode != "pre":
        yb = k.sb("yb", [128, 8, TT], BF16)
        y2 = k.sb("y2", [128, 2, TT], BF16)
        acc = k.sb("acc", [128, 8, TT])
        mg = k.sb("mg", [128, 8, TT], BF16)
        sg = [k.sb("sg%d" % i, [128, TT]) for i in range(2)]
        tm = [k.sb("tm%d" % i, [128, TT]) for i in range(2)]
        ff = k.sb("ff", [128, 32, TT], BF16)
        st = [k.sb("st%d" % i, [128, 4096]) for i in range(2)]
        wc = [k.sb("wc%d" % i, [128, 4096], BF16) for i in range(3)]
        wi = [0]
        casteng = ["act", "dve", "act"]

        def load_w(src_ap3):
            i = wi[0]
            wi[0] += 1
            s = st[i % 2]
            c = wc[i % 3]
            a, b = src_ap3.shape[1], src_ap3.shape[2]
            sv = s[:, 0:a * b].rearrange("p (a b) -> p a b", a=a)
            k.dma("sp", sv, src_ap3, w=[s])
            ce = casteng[i % 3]
            cp(k, ce, c, c[:, 0:a * b], s, s[:, 0:a * b])
            return c, c[:, 0:a * b].rearrange("p (a b) -> p a b", a=a)

    for t in range(nt):
        tsl = slice(t * TT, (t + 1) * TT)
        xc = x[t % 2]
        k.dma("sp", xc[:, :, :], xT[:, tsl].rearrange("(kt p) n -> p kt n", p=128), r=[xT], w=[xc])
        if mode == "pre":
            rmsnorm(xc, hb, None, 2)
            k.dma("sp", hT[:, tsl].rearrange("(kt p) n -> p kt n", p=128), hb[:, :, :], r=[hb], w=[hT], semb=hb)
            continue
        k.dma("sp", yb[:, :, :], yT[:, tsl].rearrange("(kt p) n -> p kt n", p=128), r=[yT], w=[yb])
        rmsnorm(xc, hb, None, 0)
        cch, gwv = load_w(gw[:, :].rearrange("(kt p) c -> p kt c", p=128))
        for j in range(2):
            p = nps()
            for kt in range(2):
                mm(k, p, p[:, :], cch, gwv[:, kt, j * 128:(j + 1) * 128], yb, yb[:, 4 + kt, :], start=(kt == 0), stop=(kt == 1))
            act(k, sg[0], sg[0][:, :], p, p[:, :], AF.Sigmoid, bias=gbc[:, j:j + 1], extra_r=[gbc])
            tt(k, "dve", y2, y2[:, j, :], yb, yb[:, 4 + j, :], sg[0], sg[0][:, :], ALU.mult)
        n = 0
        for i in range(4):
            cb, cbv = load_w(wb[i, :, :].rearrange("(kt p) c -> p kt c", p=128))
            for half in range(2):
                cg, cgv = load_w(wg[:, i * D + half * 512: i * D + (half + 1) * 512].rearrange("(kt p) c -> p kt c", p=128))
                for jj in range(4):
                    j = half * 4 + jj
                    pg = nps()
                    for kt in range(8):
                        mm(k, pg, pg[:, :], cg, cgv[:, kt, jj * 128:(jj + 1) * 128], hb, hb[:, kt, :], start=(kt == 0), stop=(kt == 7))
                    pb = nps()
                    for kt in range(2):
                        rhs_b, rhs_ap = (y2, y2[:, kt, :]) if i == 2 else (yb, yb[:, 2 * i + kt, :])
                        mm(k, pb, pb[:, :], cb, cbv[:, kt, j * 128:(j + 1) * 128], rhs_b, rhs_ap, start=(kt == 0), stop=(kt == 1))
                    s_ = sg[n % 2]
                    t_ = tm[n % 2]
                    n += 1
                    act(k, s_, s_[:, :], pg, pg[:, :], AF.Sigmoid)
                    if i == 0:
                        tt(k, "dve", acc, acc[:, j, :], pb, pb[:, :], s_, s_[:, :], ALU.mult)
                    else:
                        tt(k, "dve", t_, t_[:, :], pb, pb[:, :], s_, s_[:, :], ALU.mult)
                        if i < 3:
                            tt(k, "dve", acc, acc[:, j, :], acc, acc[:, j, :], t_, t_[:, :], ALU.add)
                        else:
                            tt(k, "dve", mg, mg[:, j, :], acc, acc[:, j, :], t_, t_[:, :], ALU.add)
        x1 = x[(t + 1) % 2] if False else acc
        for half in range(2):
            co, cov = load_w(wo[:, half * 512:(half + 1) * 512].rearrange("(kt p) c -> p kt c", p=128))
            for jj in range(4):
                j = half * 4 + jj
                p = nps()
                for kt in range(8):
                    mm(k, p, p[:, :], co, cov[:, kt, jj * 128:(jj + 1) * 128], mg, mg[:, kt, :], start=(kt == 0), stop=(kt == 7))
                tt(k, "dve", x1, x1[:, j, :], p, p[:, :], xc, xc[:, j, :], ALU.add)
        rmsnorm(x1, hb, None, 1)
        for q in range(8):
            c1, c1v = load_w(w1[:, q * 512:(q + 1) * 512].rearrange("(kt p) c -> p kt c", p=128))
            for jj in range(4):
                j = q * 4 + jj
                p = nps()
                for kt in range(8):
                    mm(k, p, p[:, :], c1, c1v[:, kt, jj * 128:(jj + 1) * 128], hb, hb[:, kt, :], start=(kt == 0), stop=(kt == 7))
                s_ = sg[n % 2]
                n += 1
                act(k, s_, s_[:, :], p, p[:, :], AF.Relu)
                act(k, ff, ff[:, j, :], s_, s_[:, :], AF.Square)
        for j in range(8):
            c2, c2v = load_w(w2[:, j * 128:(j + 1) * 128].rearrange("(kt p) c -> p kt c", p=128))
            p = nps()
            for kt in range(32):
                mm(k, p, p[:, :], c2, c2v[:, kt, :], ff, ff[:, kt, :], start=(kt == 0), stop=(kt == 31))
            tt(k, "dve", xc, xc[:, j, :], p, p[:, :], x1, x1[:, j, :], ALU.add)
        if mode == "mid":
            k.dma("sp", xo[:, tsl].rearrange("(kt p) n -> p kt n", p=128), xc[:, :, :], r=[xc], w=[xo], semb=xc)
            rmsnorm(xc, hb, None, 2)
            k.dma("sp", hT[:, tsl].rearrange("(kt p) n -> p kt n", p=128), hb[:, :, :], r=[hb], w=[hT], semb=hb)
        else:
            rmsnorm(xc, acc, None, 2)
            k.dma("sp", oT[:, tsl].rearrange("(kt p) n -> p kt n", p=128), acc[:, :, :], r=[acc], w=[oT], semb=acc)
    k.finish()
    return k.nc


TA = 256
CH = 64
NCH = TA // CH
NG = 10
FULLG = (0, 1, 2, 3, 4, 6)
(MU0, CW0, LCB, DTB, ALOG, GCOL, W0, A0, KKC, KAC, RKC, LNB, BAC, BXC, LAMC, S5D, MKC, SCL, EPSL, SIGN, EPSN, NPV) = (
    0, 10, 50, 51, 52, 53, 54, 55, 56, 57, 58, 59, 60, 61, 62, 63, 64, 65, 66, 67, 68, 69)
C_ID, C_MA, C_ML, C_RS, NCST = 0, 128, 256, 320, 320 + TA
S5W = 12 + 3 * 64


def build_A(NS, stop=None):
    k = KB()
    nt = NS // TA
    if stop is not None and (stop == 'init' or stop[0] == 'i'):
        nt = 0
    hT = k.dram("D_hT", [D, NS], BF16, "ExternalInput")
    wp = k.dram("D_wp", [D, NG * 128], F32, "ExternalInput")
    pvd = k.dram("D_pv", [128, NPV], F32, "ExternalInput")
    cstd = k.dram("D_cst", [128, NCST], F32, "ExternalInput")
    matd = k.dram("D_mats", [128, 384], F32, "ExternalInput")
    s5d = k.dram("D_s5p", [128, S5W], F32, "ExternalInput")
    yT = k.dram("D_yT", [256, NS], BF16, "ExternalOutput")

    def T(name, cols, dt=F32):
        return k.sb(name, [128, cols], dt)
    LO, UP, ALL = slice(0, 64), slice(64, 128), slice(0, 128)

    pv = T("pv", NPV)
    cst = T("cst", NCST)
    mats = T("mats", 384)
    s5p = T("s5p", S5W)
    for b_, d_ in ((pv, pvd), (cst, cstd), (mats, matd), (s5p, s5d)):
        k.dma("sp", b_[:, :], d_[:, :], r=[d_], w=[b_])
    ident = cst[:, C_ID:C_ID + 128]
    col = lambda c, P=ALL: pv[P, c:c + 1]
    k.sync_all([pv, cst, mats, s5p])

    if stop == 'i0':
        k.finish()
        return k.nc
    wpb = k.sb("wpb", [128, 8, NG * 128], BF16)
    wst = [T("wst%d" % i, NG * 128) for i in range(2)]
    for kt in range(8):
        s = wst[kt % 2]
        k.dma("sp", s[:, :], wp[kt * 128:(kt + 1) * 128, :], r=[wp], w=[s])
        cp(k, ("dve", "act")[kt % 2], wpb, wpb[:, kt, :], s, s[:, :])

    if stop == 'i1':
        k.finish()
        return k.nc
    dv = T("dv", 32)
    (D_NEGA, D_1MKA, D_LC, D_LC2, D_NMK, D_NSIGN, D_Z, D_Z2, D_T) = range(9)
    dcol = lambda c, P=ALL: dv[P, c:c + 1]
    taps = k.sb("taps", [128, NG, 4])
    cp(k, "dve", taps, taps[:, :, :], pv, pv[:, CW0:CW0 + 40].rearrange("p (g j) -> p g j", j=4))
    tt(k, "dve", taps, taps[:, :, 2], taps, taps[:, :, 2], pv, pv[:, MU0:MU0 + 10], ALU.add)
    tt(k, "dve", taps, taps[:, :, 3], taps, taps[:, :, 3], pv, pv[:, MU0:MU0 + 10], ALU.subtract)
    act(k, dv, dcol(D_NEGA), pv, col(ALOG), AF.Exp)
    ts(k, "dve", dv, dcol(D_NEGA), dv, dcol(D_NEGA), -1.0, ALU.mult)
    ts(k, "dve", dv, dcol(D_1MKA), pv, col(KAC), -1.0, ALU.mult, 1.0, ALU.add)
    ts(k, "dve", dv, dcol(D_NMK), pv, col(MKC), -1.0, ALU.mult)
    ts(k, "dve", dv, dcol(D_NSIGN), pv, col(SIGN), -1.0, ALU.mult)
    act(k, dv, dcol(D_Z), pv, col(LAMC), AF.Exp, scale=-1.0)
    ts(k, "dve", dv, dcol(D_T), dv, dcol(D_Z), -0.25, ALU.mult, 1.0 / 3.0, ALU.add)
    tt(k, "dve", dv, dcol(D_T), dv, dcol(D_T), dv, dcol(D_Z), ALU.mult)
    ts(k, "dve", dv, dcol(D_T), dv, dcol(D_T), -1.0, ALU.mult, 0.5, ALU.add)
    tt(k, "dve", dv, dcol(D_T), dv, dcol(D_T), dv, dcol(D_Z), ALU.mult)
    ts(k, "dve", dv, dcol(D_T), dv, dcol(D_T), -1.0, ALU.mult, 1.0, ALU.add)
    tt(k, "dve", dv, dcol(D_T), dv, dcol(D_T), dv, dcol(D_Z), ALU.mult)
    ts(k, "dve", dv, dcol(D_LC), dv, dcol(D_T), -8.0, ALU.mult)
    ts(k, "dve", dv, dcol(D_LC2), dv, dcol(D_T), -16.0, ALU.mult)

    if stop == 'i2':
        k.finish()
        return k.nc
    bd1 = T("bd1", 128)
    bd64 = T("bd64", 128)
    for b_, v_ in ((bd1, 1.0), (bd64, 1.0 / 64.0)):
        k.op("dve", lambda e, b_=b_: e.memset(b_[:, :], 0.0), w=[b_])
        k.op("dve", lambda e, b_=b_, v_=v_: e.memset(b_[LO, 0:64], v_), w=[b_])
        k.op("dve", lambda e, b_=b_, v_=v_: e.memset(b_[UP, 64:128], v_), w=[b_])
    if stop == 'i2a':
        k.finish()
        return k.nc
    idblk = T("idblk", 64)
    cp(k, "dve", idblk, idblk[LO, :], cst, cst[LO, C_ID:C_ID + 64])
    cp(k, "dve", idblk, idblk[UP, :], cst, cst[UP, C_ID + 64:C_ID + 128])
    if stop == 'i2b':
        k.finish()
        return k.nc
    gupp = T("gupp", 128)
    k.op("dve", lambda e: e.memset(gupp[:, 0:64], 0.0), w=[gupp])
    cp(k, "dve", gupp, gupp[:, 64:128], mats, mats[:, 256:320])
    if stop == 'i2c':
        k.finish()
        return k.nc
    DMa = k.sb("DMa", [128, NCH, 128])
    DMl = k.sb("DMl", [128, NCH, 64])
    maskA = cst[:, C_MA:C_MA + 128]
    maskL = cst[:, C_ML:C_ML + 64]
    for c in range(NCH):
        cp(k, "dve", DMa, DMa[:, c, :], cst, maskA)
        cp(k, "dve", DMl, DMl[:, c, :], cst, maskL)
    reset = cst[:, C_RS:C_RS + TA]

    if stop == 'i3':
        k.finish()
        return k.nc
    pj = [k.ps("pj%d" % i, [128, 512]) for i in range(2)]
    pm = [k.ps("pm%d" % i, [128, 512]) for i in range(2)]
    pd = [k.ps("pd%d" % i, [128, 512]) for i in range(3)]
    pq = k.ps("pq", [128, 512])
    pmi, pdi = [0], [0]

    def npm():
        pmi[0] += 1
        return pm[pmi[0] % 2]

    def npd():
        pdi[0] += 1
        return pd[pdi[0] % 3]

    s5t = k.sb("s5t", [128, 24, 4])
    S = lambda i: s5t[:, i, :]
    LR, LI, LDT = s5p[:, 0:4], s5p[:, 4:8], s5p[:, 8:12]
    BR = s5p[:, 12:76].rearrange("p (g c) -> p g c", c=16)
    BS = s5p[:, 76:140].rearrange("p (g c) -> p g c", c=16)
    CTr = s5p[:, 140:204].rearrange("p (g c) -> p g c", c=16)
    (s_dt, s_xr, s_xi, s_mag, s_s, s_c, s_sh, s_t1, s_t2, s_ar, s_ai, s_den, s_cr, s_ci, s_sci, s_sai, s_nsai, s_am1, s_rden) = range(19)
    act(k, s5t, S(s_dt), s5p, LDT, AF.Exp)
    tt(k, "dve", s5t, S(s_xr), s5p, LR, s5t, S(s_dt), ALU.mult)
    tt(k, "dve", s5t, S(s_xi), s5p, LI, s5t, S(s_dt), ALU.mult)
    act(k, s5t, S(s_mag), s5t, S(s_xr), AF.Exp)
    act(k, s5t, S(s_s), s5t, S(s_xi), AF.Sin, scale=1.0 / 8.0)
    act(k, s5t, S(s_sh), s5t, S(s_xi), AF.Sin, scale=1.0 / 16.0)
    tt(k, "dve", s5t, S(s_c), s5t, S(s_sh), s5t, S(s_sh), ALU.mult)
    ts(k, "dve", s5t, S(s_c), s5t, S(s_c), -2.0, ALU.mult, 1.0, ALU.add)
    for _ in range(3):
        tt(k, "dve", s5t, S(s_t1), s5t, S(s_c), s5t, S(s_c), ALU.mult)
        tt(k, "dve", s5t, S(s_t2), s5t, S(s_s), s5t, S(s_s), ALU.mult)
        tt(k, "dve", s5t, S(s_s), s5t, S(s_s), s5t, S(s_c), ALU.mult)
        ts(k, "dve", s5t, S(s_s), s5t, S(s_s), 2.0, ALU.mult)
        tt(k, "dve", s5t, S(s_c), s5t, S(s_t1), s5t, S(s_t2), ALU.subtract)
    tt(k, "dve", s5t, S(s_ar), s5t, S(s_mag), s5t, S(s_c), ALU.mult)
    tt(k, "dve", s5t, S(s_ai), s5t, S(s_mag), s5t, S(s_s), ALU.mult)
    if stop == 'i4':
        k.finish()
        return k.nc
    tt(k, "dve", s5t, S(s_t1), s5p, LR, s5p, LR, ALU.mult)
    tt(k, "dve", s5t, S(s_t2), s5p, LI, s5p, LI, ALU.mult)
    tt(k, "dve", s5t, S(s_den), s5t, S(s_t1), s5t, S(s_t2), ALU.add)
    k.op("dve", lambda e: e.reciprocal(out=S(s_rden), in_=S(s_den)), r=[s5t], w=[s5t])
    ts(k, "dve", s5t, S(s_am1), s5t, S(s_ar), -1.0, ALU.add)
    tt(k, "dve", s5t, S(s_t1), s5t, S(s_am1), s5p, LR, ALU.mult)
    tt(k, "dve", s5t, S(s_t2), s5t, S(s_ai), s5p, LI, ALU.mult)
    tt(k, "dve", s5t, S(s_cr), s5t, S(s_t1), s5t, S(s_t2), ALU.add)
    tt(k, "dve", s5t, S(s_cr), s5t, S(s_cr), s5t, S(s_rden), ALU.mult)
    tt(k, "dve", s5t, S(s_t1), s5t, S(s_ai), s5p, LR, ALU.mult)
    tt(k, "dve", s5t, S(s_t2), s5t, S(s_am1), s5p, LI, ALU.mult)
    tt(k, "dve", s5t, S(s_ci), s5t, S(s_t1), s5t, S(s_t2), ALU.subtract)
    tt(k, "dve", s5t, S(s_ci), s5t, S(s_ci), s5t, S(s_rden), ALU.mult)
    ts(k, "dve", s5t, S(s_sci), s5t, S(s_ci), col(SIGN), ALU.mult)
    ts(k, "dve", s5t, S(s_sai), s5t, S(s_ai), col(SIGN), ALU.mult)
    ts(k, "dve", s5t, S(s_nsai), s5t, S(s_sai), -1.0, ALU.mult)
    if stop == 'i5':
        k.finish()
        return k.nc
    Mg = k.sb("Mg", [128, 4, 64])
    CTg = k.sb("CTg", [128, 4, 64])
    k.op("dve", lambda e: e.memset(Mg[:, :, :], 0.0), w=[Mg])
    k.op("dve", lambda e: e.memset(CTg[:, :, :], 0.0), w=[CTg])
    tmpb = k.sb("tmpb", [128, 16])
    BT = k.sb("BT", [128, 4, 128])
    Pw = [k.sb("Pw%d" % g, [128, 9, 128]) for g in range(4)]
    Pn = [k.sb("Pn%d" % g, [128, 2, 128]) for g in range(4)]
    for g in range(4):
        ts(k, "dve", tmpb, tmpb[:, :], s5p, BS[:, g, :], s5t[:, s_sci, g:g + 1], ALU.mult, extra_r=[s5t])
        stt(k, Mg, Mg[:, g, 16 * g:16 * g + 16], s5p, BR[:, g, :], s5t[:, s_cr, g:g + 1], tmpb, tmpb[:, :], ALU.mult, ALU.add, extra_r=[s5t])
        ts(k, "dve", CTg, CTg[:, g, 16 * g:16 * g + 16], s5p, CTr[:, g, :], dcol(D_NSIGN), ALU.mult, extra_r=[dv])
        p = npm()
        mm(k, p, p[LO, 0:128], Mg, Mg[:, g, :], cst, ident)
        cp(k, "act", BT, BT[LO, g, :], p, p[LO, 0:128])
        A0_, At0 = Pn[g][:, 0, :], Pw[g][:, 0, :]
        ts(k, "dve", Pn[g], A0_, cst, ident, s5t[:, s_ar, g:g + 1], ALU.mult, extra_r=[s5t])
        ts(k, "dve", Pw[g], At0, cst, ident, s5t[:, s_ar, g:g + 1], ALU.mult, extra_r=[s5t])
        ts(k, "dve", Pn[g], Pn[g][LO, 0, 64:128], cst, cst[LO, C_ID:C_ID + 64], s5t[LO, s_sai, g:g + 1], ALU.mult, extra_r=[s5t])
        ts(k, "dve", Pn[g], Pn[g][UP, 0, 0:64], cst, cst[UP, C_ID + 64:C_ID + 128], s5t[UP, s_sai, g:g + 1], ALU.mult, extra_r=[s5t])
        ts(k, "dve", Pw[g], Pw[g][LO, 0, 64:128], cst, cst[LO, C_ID:C_ID + 64], s5t[LO, s_nsai, g:g + 1], ALU.mult, extra_r=[s5t])
        ts(k, "dve", Pw[g], Pw[g][UP, 0, 0:64], cst, cst[UP, C_ID + 64:C_ID + 128], s5t[UP, s_nsai, g:g + 1], ALU.mult, extra_r=[s5t])
    if stop == 'i6':
        dbg = k.dram("D_dbg", [128, 96 + 256 + 512], F32, "ExternalOutput")
        k.dma("sp", dbg[:, 0:76], s5t[:, 0:19, :].rearrange("p a b -> p (a b)"), r=[s5t], w=[dbg], semb=s5t)
        k.dma("sp", dbg[:, 96:224], Pn[0][:, 0, :], r=[Pn[0]], w=[dbg], semb=Pn[0])
        k.dma("sp", dbg[:, 224:352], Pw[0][:, 0, :], r=[Pw[0]], w=[dbg], semb=Pw[0])
        k.dma("sp", dbg[0:64, 352:864], BT[0:64, :, :].rearrange("p a b -> p (a b)"), r=[BT], w=[dbg], semb=BT)
        k.finish()
        return k.nc
    for kk in range(8):
        for g in range(4):
            a_, b_ = kk % 2, (kk + 1) % 2
            p, p2 = pm[0], pm[1]
            mm(k, p, p[:, 0:128], Pw[g], Pw[g][:, kk, :], Pn[g], Pn[g][:, a_, :])
            mm(k, p2, p2[:, 0:128], Pn[g], Pn[g][:, a_, :], Pw[g], Pw[g][:, kk, :])
            cp(k, "act", Pn[g], Pn[g][:, b_, :], p, p[:, 0:128])
            cp(k, "dve", Pw[g], Pw[g][:, kk + 1, :], p2, p2[:, 0:128])

    if stop == 'i7':
        k.finish()
        return k.nc
    k.sync_all([dv, taps, s5t, bd1, bd64, idblk, gupp, Mg, CTg, BT, wpb] + Pw)
    hbs = [k.sb("hb%d" % i, [128, 8, TA], BF16) for i in range(2)]
    praw = [T("praw%d" % g, TA + 3) for g in range(NG)]
    for g in range(NG):
        k.op("dve", lambda e, g=g: e.memset(praw[g][:, 0:3], 0.0), w=[praw[g]])
    F = {g: T("F%d" % g, TA) for g in (0, 1, 2, 3, 4, 6, 7)}
    t1, t2, t3, t4, t5 = (T("t%d" % i, TA) for i in range(1, 6))
    BE, KK_, AL, LD, LDs = (T(n, TA) for n in ("BE", "Kk", "AL", "LD", "LDs"))
    Aa, GO, BN, SZ = (T(n, TA) for n in ("Aa", "GO", "BN", "SZ"))
    lu = [T("lu%d" % i, TA) for i in range(6)]
    lh = [T("lh%d" % i, TA) for i in range(2)]
    HP = 128
    Hs = [[T("H%d_%d" % (i, g), HP + TA) for g in range(4)] for i in range(2)]
    for i in range(2):
        for g in range(4):
            k.op("dve", lambda e, i=i, g=g: e.memset(Hs[i][g][:, 0:HP], 0.0), w=[Hs[i][g]])
    CX = k.sb("CX", [128, 2, TA]); CS = k.sb("CS", [128, 2, TA])
    ECX = k.sb("ECX", [128, 2, TA]); ECS = k.sb("ECS", [128, 2, TA])
    ENS, DEND = T("ENS", TA), T("DEND", TA)
    RB = k.sb("RB", [128, 2, TA]); RBs = k.sb("RBs", [128, 2, TA])
    ATs, KTs, KD, AD = (T(n, TA) for n in ("ATs", "KTs", "KD", "AD"))
    gamC = T("gamC", NCH)
    CTc = k.sb("CTc", [128, NCH, 2])
    dma_t = k.sb("dma_t", [128, NCH, 128]); dml_t = k.sb("dml_t", [128, NCH, 64])
    QT = k.sb("QT", [128, NCH, 2, 64]); QQ = k.sb("QQ", [128, NCH, 64]); NA = k.sb("NA", [128, NCH, 64])
    SK = k.sb("SK", [128, NCH, 128])
    TM1 = k.sb("TM1", [128, NCH, 128]); TM2 = k.sb("TM2", [128, NCH, 128])
    SX0, SU0, SWT = (k.sb(n, [128, NCH, 64]) for n in ("SX0", "SU0", "SWT"))
    SU = [T("SUc%d" % i, 64) for i in range(2)]
    Ast = [T("Ast%d" % i, 64) for i in range(2)]
    k.op("dve", lambda e: e.memset(Ast[0][:, :], 0.0), w=[Ast[0]])
    OT, CEN, SQ, RS_, RT_ = (T(n, TA) for n in ("OT", "CEN", "SQ", "RS", "RT"))
    YA = [T("YA%d" % i, TA, BF16) for i in range(2)]
    YB = [k.sb("YB%d" % i, [128, 2, TA], BF16) for i in range(2)]
    sti = 0
    c4 = lambda ap: ap.rearrange("p (c t) -> p c t", t=CH)

    def l2n(src_b, src_ap, P, out_b, out_ap, sclcol):
        act(k, t1, t1[P, :], src_b, src_ap, AF.Square)
        p = npm()
        if P == ALL:
            mm(k, p, p[:, 0:TA], bd1, bd1[:, :], t1, t1[:, :])
        else:
            mm(k, p, p[P, 0:TA], bd1, bd1[P, P], t1, t1[P, :])
        rsqrt_(k, t3, t3[P, :], p, p[P, 0:TA], t2, t2[P, :], 1.0, pv, col(EPSL, P))
        stt(k, out_b, out_ap, src_b, src_ap, sclcol, t3, t3[P, :], ALU.mult, ALU.mult)

    for t in range(nt):
        tsl = slice(t * TA, (t + 1) * TA)
        hb = hbs[t % 2]
        k.dma("sp", hb[:, :, :], hT[:, tsl].rearrange("(kt p) n -> p kt n", p=128), r=[hT], w=[hb])
        for g in range(NG):
            M = 128 if g in FULLG else 64
            p = pj[g % 2]
            for kt in range(8):
                mm(k, p, p[0:M, 0:TA], wpb, wpb[:, kt, g * 128:g * 128 + M], hb, hb[:, kt, :], start=(kt == 0), stop=(kt == 7))
            cp(k, "act", praw[g], praw[g][0:M, 3:3 + TA], p, p[0:M, 0:TA])
        if stop == 'proj':
            continue
        for g, ntap, M in ((0, 4, 128), (1, 4, 128), (2, 4, 128), (3, 2, 128), (4, 2, 128), (6, 2, 128), (7, 4, 64)):
            P_, o = praw[g], F[g]
            if g == 7:
                ts(k, "dve", o, o[0:M, :], P_, P_[0:M, 3:3 + TA], taps[0:M, g, 3:4], ALU.mult, col(LCB, slice(0, M)), ALU.add)
            else:
                ts(k, "dve", o, o[0:M, :], P_, P_[0:M, 3:3 + TA], taps[0:M, g, 3:4], ALU.mult)
            for j in range(1, ntap):
                stt(k, o, o[0:M, :], P_, P_[0:M, 3 - j:3 - j + TA], taps[0:M, g, 3 - j:4 - j], o, o[0:M, :], ALU.mult, ALU.add)
            cp(k, "act", P_, P_[0:M, 0:3], P_, P_[0:M, TA:TA + 3])
        if stop == 'fir':
            continue
        Q, V = F[0], F[2]
        for g in (0, 1, 2):
            act(k, F[g], F[g][LO, :], F[g], F[g][LO, :], AF.Silu)
        act(k, SZ, SZ[LO, :], F[3], F[3][LO, :], AF.Silu)
        l2n(F[0], F[0][LO, :], LO, F[0], F[0][LO, :], col(SCL, LO))
        ts(k, "dve", t4, t4[:, :], F[1], F[1][:, :], col(KKC), ALU.mult)
        l2n(t4, t4[:, :], ALL, BE, BE[:, :], 1.0)
        act(k, t5, t5[LO, :], F[4], F[4][LO, :], AF.Sigmoid)
        tt(k, "dve", KK_, KK_[LO, :], BE, BE[LO, :], t5, t5[LO, :], ALU.mult)
        act(k, t1, t1[LO, :], praw[5], praw[5][LO, 3:3 + TA], AF.Exp, bias=col(DTB, LO))
        act(k, t2, t2[LO, :], t1, t1[LO, :], AF.Ln, bias=1.0)
        ts(k, "dve", LD, LD[LO, :], t2, t2[LO, :], dcol(D_NEGA, LO), ALU.mult)
        act(k, t1, t1[LO, :], LD, LD[LO, :], AF.Exp)
        stt(k, AL, AL[LO, :], KK_, KK_[LO, :], -1.0, t1, t1[LO, :], ALU.mult, ALU.mult)
        if stop == 'gdnprep':
            continue
        act(k, t1, t1[UP, :], F[3], F[3][UP, :], AF.Tanh)
        p = npm()
        mm(k, p, p[UP, 0:TA], mats, mats[UP, 128:192], t1, t1[UP, :])
        act(k, t2, t2[UP, :], p, p[UP, 0:TA], AF.Sigmoid, bias=col(W0, UP))
        ts(k, "dve", LD, LD[UP, :], t2, t2[UP, :], -float(np.exp(-0.5)), ALU.mult)
        p = npm()
        mm(k, p, p[UP, 0:TA], mats, mats[UP, 192:256], F[4], F[4][UP, :])
        act(k, Aa, Aa[UP, :], p, p[UP, 0:TA], AF.Sigmoid, bias=col(A0, UP))
        act(k, t3, t3[:, :], F[6], F[6][:, :], AF.Sigmoid)
        p = npm()
        mm(k, p, p[:, 0:TA], gupp, gupp[:, :], t3, t3[:, :])
        cp(k, "act", GO, GO[UP, :], p, p[UP, 0:TA])
        ts(k, "dve", t4, t4[UP, :], Aa, Aa[UP, :], col(KAC, UP), ALU.mult, dcol(D_1MKA, UP), ALU.add)
        tt(k, "dve", KK_, KK_[UP, :], F[1], F[1][UP, :], t4, t4[UP, :], ALU.mult)
        stt(k, AL, AL[UP, :], BE, BE[UP, :], -1.0, Aa, Aa[UP, :], ALU.mult, ALU.mult)
        stt(k, t5, t5[UP, :], F[0], F[0][UP, :], col(RKC, UP), KK_, KK_[UP, :], ALU.mult, ALU.mult)
        p = npm()
        mm(k, p, p[UP, 0:TA], bd1, bd1[UP, UP], t5, t5[UP, :])
        tt(k, "dve", BN, BN[UP, :], p, p[UP, 0:TA], V, V[UP, :], ALU.mult)
        if stop == 'rwkvprep':
            continue
        U7 = F[7]
        p = npm()
        mm(k, p, p[LO, 0:TA], mats, mats[LO, 0:64], U7, U7[LO, :])
        mm(k, p, p[LO, TA:2 * TA], mats, mats[LO, 64:128], U7, U7[LO, :])
        act(k, lu[0], lu[0][LO, :], p, p[LO, 0:TA], AF.Sigmoid, bias=col(BAC, LO))
        act(k, lu[1], lu[1][LO, :], p, p[LO, TA:2 * TA], AF.Sigmoid, bias=col(BXC, LO))
        act(k, lu[2], lu[2][LO, :], lu[0], lu[0][LO, :], AF.Exp, scale=dcol(D_LC, LO))
        act(k, lu[3], lu[3][LO, :], lu[0], lu[0][LO, :], AF.Exp, scale=dcol(D_LC2, LO))
        ts(k, "dve", lu[3], lu[3][LO, :], lu[3], lu[3][LO, :], -1.0, ALU.mult, 1.0, ALU.add)
        act(k, lu[3], lu[3][LO, :], lu[3], lu[3][LO, :], AF.Sqrt)
        tt(k, "dve", lu[1], lu[1][LO, :], lu[1], lu[1][LO, :], U7, U7[LO, :], ALU.mult)
        tt(k, "dve", lu[1], lu[1][LO, :], lu[1], lu[1][LO, :], lu[3], lu[3][LO, :], ALU.mult)
        hcur, hprev = lh[t % 2], lh[(t + 1) % 2]
        init = 0.0 if t == 0 else hprev[LO, TA - 1:TA]
        k.op("dve", lambda e, init=init, hcur=hcur: e.tensor_tensor_scan(out=hcur[LO, :], data0=lu[2][LO, :], data1=lu[1][LO, :],
                                                                         initial=init, op0=ALU.mult, op1=ALU.add),
             r=[lu[2], lu[1], hprev], w=[hcur])
        act(k, lu[4], lu[4][LO, :], praw[8], praw[8][LO, 3:3 + TA], AF.Gelu_apprx_tanh)
        yb_ = YB[t % 2]
        tt(k, "dve", yb_, yb_[LO, 0, :], hcur, hcur[LO, :], lu[4], lu[4][LO, :], ALU.mult)
        if stop == 'lru':
            continue
        u9 = praw[9]
        Hc, Hp = Hs[t % 2], Hs[(t + 1) % 2]
        for g in range(4):
            p = npm()
            mm(k, p, p[:, 0:TA], BT, BT[LO, g, :], u9, u9[LO, 3:3 + TA])
            cp(k, "act", Hc[g], Hc[g][:, HP:HP + TA], p, p[:, 0:TA])
            if t > 0:
                p = npm()
                mm(k, p, p[:, 0:2], Pw[g], Pw[g][:, 0, :], Hp[g], Hp[g][:, HP + TA - 2:HP + TA])
                tt(k, "dve", Hc[g], Hc[g][:, HP:HP + 1], p, p[:, 1:2], Hc[g], Hc[g][:, HP:HP + 1], ALU.add)
        for kk in range(8):
            s = 1 << kk
            for g in range(4):
                p = npm()
                mm(k, p, p[:, 0:TA], Pw[g], Pw[g][:, kk, :], Hc[g], Hc[g][:, HP - s:HP - s + TA])
                tt(k, "dve", Hc[g], Hc[g][:, HP:HP + TA], p, p[:, 0:TA], Hc[g], Hc[g][:, HP:HP + TA], ALU.add)
        p = npm()
        for g in range(4):
            mm(k, p, p[LO, 0:TA], CTg, CTg[:, g, :], Hc[g], Hc[g][:, HP:HP + TA], start=(g == 0), stop=(g == 3))
        stt(k, lu[5], lu[5][LO, :], u9, u9[LO, 3:3 + TA], col(S5D, LO), p, p[LO, 0:TA], ALU.mult, ALU.add)
        act(k, yb_, yb_[LO, 1, :], lu[5], lu[5][LO, :], AF.Gelu_apprx_tanh)
        if stop == 's5':
            continue
        ts(k, "dve", LDs, LDs[:, :], LD, LD[:, :], col(MKC), ALU.mult)
        for src, dst in ((LD, CX), (LDs, CS)):
            k.op("dve", lambda e, src=src, dst=dst: e.tensor_tensor_scan(out=dst[:, 1, :], data0=reset, data1=src[:, :], initial=0.0,
                                                                         op0=ALU.mult, op1=ALU.add), r=[src], w=[dst])
            tt(k, "dve", dst, dst[:, 0, :], dst, dst[:, 1, :], src, src[:, :], ALU.subtract)
        act(k, ECX, ECX[:, :, :], CX, CX[:, :, :], AF.Exp)
        act(k, ECS, ECS[:, :, :], CS, CS[:, :, :], AF.Exp)
        act(k, ENS, ENS[:, :], CS, CS[:, 1, :], AF.Exp, scale=-1.0)
        clast = c4(CX[:, 1, :])[:, :, CH - 1:CH]
        tt(k, "dve", DEND, c4(DEND[:, :]), CX, clast.broadcast_to([128, NCH, CH]), CX, c4(CX[:, 1, :]), ALU.subtract)
        act(k, DEND, DEND[:, :], DEND, DEND[:, :], AF.Exp)
        act(k, gamC, gamC[:, :].unsqueeze(2), CX, clast, AF.Exp)
        tt(k, "dve", RB, RB[:, 0, :], BE, BE[:, :], ECX, ECX[:, 0, :], ALU.mult)
        tt(k, "dve", RB, RB[:, 1, :], Q, Q[:, :], ECX, ECX[:, 1, :], ALU.mult)
        tt(k, "dve", RBs, RBs[:, 0, :], BE, BE[:, :], ECS, ECS[:, 0, :], ALU.mult)
        tt(k, "dve", RBs, RBs[:, 1, :], Q, Q[:, :], ECS, ECS[:, 1, :], ALU.mult)
        tt(k, "dve", ATs, ATs[:, :], AL, AL[:, :], ENS, ENS[:, :], ALU.mult)
        tt(k, "dve", KTs, KTs[:, :], KK_, KK_[:, :], ENS, ENS[:, :], ALU.mult)
        tt(k, "dve", KD, KD[:, :], KK_, KK_[:, :], DEND, DEND[:, :], ALU.mult)
        tt(k, "dve", AD, AD[:, :], AL, AL[:, :], DEND, DEND[:, :], ALU.mult)
        p = npd()
        for c in range(NCH):
            for w_ in range(2):
                o_ = (c * 2 + w_) * 2
                mm(k, p, p[LO, o_:o_ + 2], CX, CX[LO, w_, c * CH:(c + 1) * CH], cst, cst[LO, C_ID:C_ID + 2])
        cp(k, "act", CTc, CTc[LO, :, :], p, p[LO, 0:4 * NCH].rearrange("p (c w j) -> p c w j", w=2, j=2)[:, :, :, 0])
        cxv = CX[LO, :, :].rearrange("p w (c t) -> p c w t", t=CH)
        dmv = dma_t[LO, :, :].rearrange("p c (w t) -> p c w t", t=CH)
        tt(k, "dve", dma_t, dmv, CX, cxv, CTc, CTc[LO, :, 1:2].unsqueeze(3).broadcast_to([64, NCH, 2, CH]), ALU.subtract)
        ts(k, "dve", dma_t, dma_t[LO, :, :], dma_t, dma_t[LO, :, :], 0.0, ALU.min)
        act(k, dma_t, dma_t[LO, :, :], dma_t, dma_t[LO, :, :], AF.Exp)
        tt(k, "dve", DMa, DMa[LO, :, :], dma_t, dma_t[LO, :, :], cst, maskA[LO, :].unsqueeze(1).broadcast_to([64, NCH, 128]), ALU.mult)
        tt(k, "dve", dml_t, dml_t[LO, :, :], CTc, CTc[LO, :, 0:1].broadcast_to([64, NCH, CH]), CX, c4(CX[LO, 1, :]), ALU.subtract)
        ts(k, "dve", dml_t, dml_t[LO, :, :], dml_t, dml_t[LO, :, :], 0.0, ALU.min)
        act(k, dml_t, dml_t[LO, :, :], dml_t, dml_t[LO, :, :], AF.Exp)
        tt(k, "dve", DMl, DMl[LO, :, :], dml_t, dml_t[LO, :, :], cst, maskL[LO, :].unsqueeze(1).broadcast_to([64, NCH, CH]), ALU.mult)
        if stop == 'dplr_a':
            continue
        pA, pK, pM = npd(), npd(), npd()
        for c in range(NCH):
            cs = slice(c * CH, (c + 1) * CH)
            for P in (LO, UP):
                mm(k, pA, pA[P, c * 128:(c + 1) * 128], ATs, ATs[P, cs], RBs, RBs[P, :, cs])
                mm(k, pK, pK[P, c * 128:(c + 1) * 128], KTs, KTs[P, cs], RBs, RBs[P, :, cs])
                mm(k, pM, pM[P, c * 64:(c + 1) * 64], RBs, RBs[P, 0, cs], ATs, ATs[P, cs])
        pAv = pA[:, :].rearrange("p (c x) -> p c x", x=128)
        tt(k, "dve", QT, QT[:, :, 0, :], pA, pAv[:, :, 0:64], DMa, DMa[:, :, 0:64], ALU.mult)
        tt(k, "dve", NA, NA[:, :, :], pA, pAv[:, :, 64:128], DMa, DMa[:, :, 64:128], ALU.mult)
        tt(k, "dve", QT, QT[:, :, 1, :], QT, QT[:, :, 0, :], idblk, idblk[:, :].unsqueeze(1).broadcast_to([128, NCH, 64]), ALU.add)
        tt(k, "dve", SK, SK[:, :, :], pK, pK[:, :].rearrange("p (c x) -> p c x", x=128), DMa, DMa[:, :, :], ALU.mult)
        tt(k, "dve", QQ, QQ[:, :, :], pM, pM[:, 0:NCH * 64].rearrange("p (c x) -> p c x", x=64), DMl, DMl[:, :, :], ALU.mult)
        if stop == 'dplr_b':
            continue
        pT1, pT2 = npd(), npd()
        for c in range(NCH):
            cs = slice(c * CH, (c + 1) * CH)
            for P in (LO, UP):
                idb = cst[P, C_ID + P.start:C_ID + P.start + 64]
                mm(k, pT1, pT1[P, c * 128:c * 128 + 64], V, V[P, cs], cst, idb)
                mm(k, pT1, pT1[P, c * 128 + 64:c * 128 + 128], RB, RB[P, 0, cs], cst, idb)
                mm(k, pT2, pT2[P, c * 128:c * 128 + 64], KD, KD[P, cs], cst, idb)
                mm(k, pT2, pT2[P, c * 128 + 64:c * 128 + 128], AD, AD[P, cs], cst, idb)
        cp(k, "act", TM1, TM1[:, :, :], pT1, pT1[:, :].rearrange("p (c x) -> p c x", x=128))
        cp(k, "act", TM2, TM2[:, :, :], pT2, pT2[:, :].rearrange("p (c x) -> p c x", x=128))
        if stop == 'dplr_c':
            continue
        for i in range(1, 7):
            pN, pS = npd(), npd()
            for c in range(NCH):
                for P in (LO, UP):
                    if i == 1:
                        mm(k, pN, pN[P, c * 128:c * 128 + 64], QQ, QQ[P, c, :], QT, QT[P, c, 0, :])
                    elif i == 6:
                        mm(k, pN, pN[P, c * 128 + 64:c * 128 + 128], QQ, QQ[P, c, :], QT, QT[P, c, 1, :])
                    else:
                        mm(k, pN, pN[P, c * 128:c * 128 + 128], QQ, QQ[P, c, :], QT, QT[P, c, :, :])
                    if i <= 5:
                        mm(k, pS, pS[P, c * 64:c * 64 + 64], QT, QT[P, c, 0, :], QQ, QQ[P, c, :])
            pNv = pN[:, :].rearrange("p (c x) -> p c x", x=128)
            if i >= 2:
                tt(k, "dve", QT, QT[:, :, 1, :], pN, pNv[:, :, 64:128], QT, QT[:, :, 1, :], ALU.add)
            if i <= 5:
                cp(k, "act", QT, QT[:, :, 0, :], pN, pNv[:, :, 0:64])
                cp(k, "act", QQ, QQ[:, :, :], pS, pS[:, 0:NCH * 64].rearrange("p (c x) -> p c x", x=64))
        if stop == 'neumann':
            continue
        pX = npd()
        for c in range(NCH):
            for P in (LO, UP):
                mm(k, pX, pX[P, c * 64:c * 64 + 64], SK, SK[P, c, 0:64], TM1, TM1[P, c, 0:64])
        cp(k, "act", SX0, SX0[:, :, :], pX, pX[:, 0:NCH * 64].rearrange("p (c x) -> p c x", x=64))
        pU, pW = npd(), npd()
        for c in range(NCH):
            for P in (LO, UP):
                mm(k, pU, pU[P, c * 64:c * 64 + 64], QT, QT[P, c, 1, :], SX0, SX0[P, c, :])
                mm(k, pW, pW[P, c * 64:c * 64 + 64], TM1, TM1[P, c, 64:128], QT, QT[P, c, 1, :])
        cp(k, "act", SU0, SU0[:, :, :], pU, pU[:, 0:NCH * 64].rearrange("p (c x) -> p c x", x=64))
        cp(k, "dve", SWT, SWT[:, :, :], pW, pW[:, 0:NCH * 64].rearrange("p (c x) -> p c x", x=64))
        if stop == 'dplr_d':
            continue
        for c in range(NCH):
            cs = slice(c * CH, (c + 1) * CH)
            A_, An = Ast[sti % 2], Ast[(sti + 1) % 2]
            su = SU[sti % 2]
            sti += 1
            pu_, ps_, po_ = pd[0], pd[1], pd[2]
            for P in (LO, UP):
                mm(k, pu_, pu_[P, 0:64], SWT, SWT[P, c, :], A_, A_[P, :])
            tt(k, "dve", su, su[:, :], pu_, pu_[:, 0:64], SU0, SU0[:, c, :], ALU.add)
            for P in (LO, UP):
                mm(k, ps_, ps_[P, 0:64], TM2, TM2[P, c, 64:128], su, su[P, :])
                mm(k, ps_, ps_[P, 64:128], TM2, TM2[P, c, 0:64], TM1, TM1[P, c, 0:64])
            for P in (LO, UP):
                mm(k, po_, po_[P, 0:64], A_, A_[P, :], RB, RB[P, 1, cs])
                mm(k, po_, po_[P, 64:128], su, su[P, :], NA, NA[P, c, :])
                mm(k, po_, po_[P, 128:192], TM1, TM1[P, c, 0:64], SK, SK[P, c, 64:128])
            stt(k, An, An[:, :], A_, A_[:, :], gamC[:, c:c + 1], ps_, ps_[:, 0:64], ALU.mult, ALU.add, extra_r=[gamC])
            tt(k, "dve", An, An[:, :], ps_, ps_[:, 64:128], An, An[:, :], ALU.add)
            cp(k, "dve", OT, OT[:, cs], po_, po_[:, 0:64])
            tt(k, "dve", OT, OT[:, cs], po_, po_[:, 64:128], OT, OT[:, cs], ALU.add)
            tt(k, "dve", OT, OT[:, cs], po_, po_[:, 128:192], OT, OT[:, cs], ALU.add)
        if stop == 'seq':
            continue
        p = npm()
        mm(k, p, p[:, 0:TA], bd64, bd64[:, :], OT, OT[:, :])
        stt(k, CEN, CEN[:, :], p, p[:, 0:TA], dcol(D_NMK), OT, OT[:, :], ALU.mult, ALU.add)
        act(k, SQ, SQ[:, :], CEN, CEN[:, :], AF.Square)
        p = npm()
        mm(k, p, p[:, 0:TA], bd64, bd64[:, :], SQ, SQ[:, :])
        rsqrt_(k, RS_, RS_[:, :], p, p[:, 0:TA], RT_, RT_[:, :], 1.0, pv, col(EPSN))
        stt(k, CEN, CEN[:, :], CEN, CEN[:, :], col(GCOL), RS_, RS_[:, :], ALU.mult, ALU.mult)
        ya = YA[t % 2]
        tt(k, "dve", ya, ya[LO, :], CEN, CEN[LO, :], SZ, SZ[LO, :], ALU.mult)
        stt(k, CEN, CEN[UP, :], CEN, CEN[UP, :], col(LNB, UP), BN, BN[UP, :], ALU.add, ALU.add)
        tt(k, "dve", ya, ya[UP, :], CEN, CEN[UP, :], GO, GO[UP, :], ALU.mult)
        if stop == 'post':
            continue
        k.dma("sp", yT[0:64, tsl], ya[LO, :], r=[ya], w=[yT], semb=ya)
        k.dma("sp", yT[192:256, tsl], ya[UP, :], r=[ya], w=[yT], semb=ya)
        k.dma("sp", yT[64:192, tsl].rearrange("(b p) n -> p b n", p=64), yb_[LO, :, :], r=[yb_], w=[yT], semb=yb_)
    k.finish()
    return k.nc


def _consts():
    c = np.zeros((128, NCST), np.float32)
    c[:, C_ID:C_ID + 128] = np.eye(128, dtype=np.float32)
    s = (np.arange(128) % 64)[:, None]
    t = np.arange(64)[None, :]
    c[:, C_MA:C_MA + 64] = (s < t)
    c[:, C_MA + 64:C_MA + 128] = (s <= t)
    c[:, C_ML:C_ML + 64] = (s > t)
    r = np.ones(TA, np.float32)
    r[::CH] = 0.0
    c[:, C_RS:C_RS + TA] = r[None, :]
    return c


def pack_A(inp, l, hg):
    h0 = hg * 64
    w = inp["w_in"][l]
    LOs, UPs = slice(0, 64), slice(64, 128)
    wp = np.zeros((D, NG, 128), np.float32)
    R0 = 1800
    wp[:, 0, LOs] = w[:, 0 + h0:0 + h0 + 64];       wp[:, 0, UPs] = w[:, R0 + h0:R0 + h0 + 64]
    wp[:, 1, LOs] = w[:, 256 + h0:256 + h0 + 64];   wp[:, 1, UPs] = w[:, R0 + 256 + h0:R0 + 256 + h0 + 64]
    wp[:, 2, LOs] = w[:, 512 + h0:512 + h0 + 64];   wp[:, 2, UPs] = w[:, R0 + 512 + h0:R0 + 512 + h0 + 64]
    wp[:, 3, LOs] = w[:, 768 + h0:768 + h0 + 64];   wp[:, 3, UPs] = w[:, R0 + 768:R0 + 832]
    wp[:, 4, LOs] = w[:, 1024 + hg:1024 + hg + 1];  wp[:, 4, UPs] = w[:, R0 + 832:R0 + 896]
    wp[:, 5, LOs] = w[:, 1028 + hg:1028 + hg + 1]
    wp[:, 6, :] = w[:, R0 + 896:R0 + 1024]
    wp[:, 7, LOs] = w[:, 1032 + h0:1032 + h0 + 64]
    wp[:, 8, LOs] = w[:, 1288 + h0:1288 + h0 + 64]
    wp[:, 9, LOs] = w[:, 1544 + h0:1544 + h0 + 64]
    pv = np.zeros((128, NPV), np.float32)
    mu = inp["rwkv_mu"][l]
    pv[UPs, MU0 + 0] = mu[0 + h0:0 + h0 + 64]
    pv[UPs, MU0 + 1] = mu[256 + h0:256 + h0 + 64]
    pv[UPs, MU0 + 2] = mu[512 + h0:512 + h0 + 64]
    pv[UPs, MU0 + 3] = mu[768:832]
    pv[UPs, MU0 + 4] = mu[832:896]
    pv[:, MU0 + 6] = mu[896:1024]
    cw = np.zeros((128, NG, 4), np.float32)
    cw[:, :, 3] = 1.0
    gcw = inp["gdn_conv_w"][l]
    for g in range(3):
        cw[LOs, g, :] = gcw[:, g * 256 + h0:g * 256 + h0 + 64].T
    cw[LOs, 7, :] = inp["lru_conv_w"][l][:, h0:h0 + 64].T
    pv[:, CW0:CW0 + 40] = cw.reshape(128, 40)
    pv[LOs, LCB] = inp["lru_conv_b"][l][h0:h0 + 64]
    pv[LOs, DTB] = inp["gdn_dt_bias"][l][hg]
    pv[LOs, ALOG] = inp["gdn_a_log"][l][hg]
    pv[LOs, GCOL] = inp["gdn_norm_g"][l]
    pv[UPs, GCOL] = inp["rwkv_ln_g"][l][h0:h0 + 64]
    pv[UPs, W0] = inp["rwkv_w0"][l][h0:h0 + 64]
    pv[UPs, A0] = inp["rwkv_a0"][l][h0:h0 + 64]
    pv[LOs, KKC] = 1.0
    pv[UPs, KKC] = inp["rwkv_k_k"][l][h0:h0 + 64]
    pv[UPs, KAC] = inp["rwkv_k_a"][l][h0:h0 + 64]
    pv[UPs, RKC] = inp["rwkv_r_k"][l][hg]
    pv[UPs, LNB] = inp["rwkv_ln_b"][l][h0:h0 + 64]
    pv[LOs, BAC] = inp["lru_b_a"][l][h0:h0 + 64]
    pv[LOs, BXC] = inp["lru_b_x"][l][h0:h0 + 64]
    pv[LOs, LAMC] = inp["lru_lambda"][l][h0:h0 + 64]
    pv[LOs, S5D] = inp["s5_d"][l][h0:h0 + 64]
    pv[UPs, MKC] = 1.0
    pv[LOs, SCL] = 0.125
    pv[UPs, SCL] = 1.0
    pv[:, EPSL] = EPS
    pv[LOs, SIGN] = -1.0
    pv[UPs, SIGN] = 1.0
    pv[LOs, EPSN] = EPS
    pv[UPs, EPSN] = 64e-5
    mats = np.zeros((128, 384), np.float32)
    mats[LOs, 0:64] = inp["lru_w_a"][l][hg]
    mats[LOs, 64:128] = inp["lru_w_x"][l][hg]
    mats[UPs, 128:192] = inp["rwkv_w_up"][l][:, h0:h0 + 64]
    mats[UPs, 192:256] = inp["rwkv_a_up"][l][:, h0:h0 + 64]
    mats[:, 256:320] = inp["rwkv_g_up"][l][:, h0:h0 + 64]
    s5 = np.zeros((128, S5W), np.float32)
    gs = slice(4 * hg, 4 * hg + 4)
    lr, li = inp["s5_lambda_re"][l][gs], inp["s5_lambda_im"][l][gs]
    s5[LOs, 0:4] = lr.T; s5[UPs, 0:4] = lr.T
    s5[LOs, 4:8] = li.T; s5[UPs, 4:8] = li.T
    s5[:, 8:12] = inp["s5_log_dt"][l][gs][None, :]
    bre, bim = inp["s5_b_re"][l][gs], inp["s5_b_im"][l][gs]
    BR = np.concatenate([bre.transpose(1, 0, 2), bim.transpose(1, 0, 2)], 0)
    BS = np.concatenate([bim.transpose(1, 0, 2), bre.transpose(1, 0, 2)], 0)
    cre, cim = inp["s5_c_re"][l][gs], inp["s5_c_im"][l][gs]
    CT = np.concatenate([cre.transpose(2, 0, 1), cim.transpose(2, 0, 1)], 0)
    s5[:, 12:76] = BR.reshape(128, 64)
    s5[:, 76:140] = BS.reshape(128, 64)
    s5[:, 140:204] = CT.reshape(128, 64)
    return {"D_wp": np.ascontiguousarray(wp.reshape(D, NG * 128)), "D_pv": pv, "D_cst": _consts(), "D_mats": mats, "D_s5p": s5}


_NC = {}


def _get(kind, *a):
    key = (kind,) + a
    if key not in _NC:
        _NC[key] = build_A(*a) if kind == "A" else build_B(*a)
    return _NC[key]


def kernel(**inp):
    inp = {k_: np.asarray(v) for k_, v in inp.items()}
    x = inp["x"]
    NTB = SEQ // 4
    cores = list(range(8))
    xT = [np.ascontiguousarray(x[c // 4, (c % 4) * NTB:(c % 4 + 1) * NTB].T) for c in cores]
    res = run_bass_kernel_spmd(_get("B", NTB, "pre"), [{"D_xT": xT[c], "D_gn": inp["norm1_g"][0]} for c in cores], core_ids=cores).results
    hT = [np.concatenate([res[b * 4 + j]["D_hT"] for j in range(4)], axis=1) for b in range(NB)]
    out = np.zeros((NB, SEQ, D), np.float32)
    for l in range(DEPTH):
        packs = [pack_A(inp, l, hg) for hg in range(4)]
        resA = run_bass_kernel_spmd(_get("A", SEQ), [dict(packs[c % 4], D_hT=hT[c // 4]) for c in cores], core_ids=cores).results
        Y = []
        for b in range(NB):
            y = np.zeros((D, SEQ), ml_dtypes.bfloat16)
            for hg in range(4):
                r = resA[b * 4 + hg]["D_yT"]
                for br in range(4):
                    y[br * 256 + hg * 64:br * 256 + hg * 64 + 64] = r[br * 64:(br + 1) * 64]
            Y.append(y)
        last = (l == DEPTH - 1)
        common = {"D_wg": np.ascontiguousarray(inp["w_in"][l][:, 2824:]), "D_wb": inp["w_branch"][l], "D_wo": inp["w_out"][l],
                  "D_w1": inp["mlp_w1"][l], "D_w2": inp["mlp_w2"][l], "D_gw": inp["s5_glu_w"][l], "D_gb": inp["s5_glu_b"][l],
                  "D_g1": inp["norm1_g"][l], "D_g2": inp["norm2_g"][l],
                  "D_gn": inp["final_norm_g"] if last else inp["norm1_g"][l + 1]}
        resB = run_bass_kernel_spmd(_get("B", NTB, "fin" if last else "mid"),
                                    [dict(common, D_xT=xT[c], D_yT=np.ascontiguousarray(Y[c // 4][:, (c % 4) * NTB:(c % 4 + 1) * NTB]))
                                     for c in cores], core_ids=cores).results
        if last:
            for c in cores:
                out[c // 4, (c % 4) * NTB:(c % 4 + 1) * NTB] = resB[c]["D_oT"].T
        else:
            xT = [resB[c]["D_xo"] for c in cores]
            hT = [np.concatenate([resB[b * 4 + j]["D_hT"] for j in range(4)], axis=1) for b in range(NB)]
    return out
```
